# Optimizing a Trainium2 kernel written in Bass

```python
import jax, jax.numpy as jnp
from jax import lax
import numpy as np

D_MODEL = 1024
BATCH = 2
SEQ = 16384
DEPTH = 2

NSA_HEADS = 8
NSA_HEAD_DIM = 64
NSA_KV_HEADS = 2
NSA_GROUP = NSA_HEADS // NSA_KV_HEADS
NSA_WIDTH = NSA_HEADS * NSA_HEAD_DIM
KV_WIDTH = NSA_KV_HEADS * NSA_HEAD_DIM
CMP_LEN = 32
CMP_STRIDE = 16
CMP_HIDDEN = 256
SLC_LEN = 64
SLC_TOPN = 16
WINDOW = 512
Q_BLOCK = 128
RET_HEADS = 4
RET_HEAD_DIM = 128
RET_WIDTH = RET_HEADS * RET_HEAD_DIM
RET_CHUNK = 128
MEM_LEN = 256
MEM_HEADS = 4
MEM_HEAD_DIM = D_MODEL // MEM_HEADS
D_FF = 2816
EPS = 1e-6
NEG = -1e30

IN_SIZES = (NSA_WIDTH,
            KV_WIDTH, KV_WIDTH,
            KV_WIDTH, KV_WIDTH,
            KV_WIDTH, KV_WIDTH,
            NSA_HEADS * 3,
            RET_WIDTH, RET_WIDTH, RET_WIDTH, RET_WIDTH)
IN_WIDTH = sum(IN_SIZES)
IN_OFFSETS = tuple(int(o) for o in np.cumsum(IN_SIZES)[:-1])

kernel_name = "hymba_nsa_retention_macaron"


def rms_norm(x, g):
    xf = x.astype(jnp.float32)
    y = xf * lax.rsqrt(jnp.mean(xf * xf, axis=-1, keepdims=True) + EPS)
    return (y * g.astype(jnp.float32)).astype(x.dtype)


def swiglu(x, w_gate, w_up, w_down):
    return (jax.nn.silu(x @ w_gate) * (x @ w_up)) @ w_down


def masked_softmax(s, mask):
    s = jnp.where(mask, s, NEG)
    m = jnp.max(s, axis=-1, keepdims=True)
    p = jnp.where(mask, jnp.exp(s - m), 0.0)
    return p / jnp.maximum(jnp.sum(p, axis=-1, keepdims=True), 1e-30)


def alibi_slopes(n):
    return jnp.asarray(2.0 ** (-8.0 * np.arange(1, n + 1) / n), dtype=jnp.float32)


def compress(kv, pe, w1, w2):
    B, T, G, dh = kv.shape
    nc = (T - CMP_LEN) // CMP_STRIDE + 1
    idx = np.arange(nc)[:, None] * CMP_STRIDE + np.arange(CMP_LEN)[None, :]
    blocks = kv[:, idx] + pe[None, None, :, None, :]
    flat = blocks.transpose(0, 1, 3, 2, 4).reshape(B, nc, G, CMP_LEN * dh)
    return jax.nn.silu(flat @ w1) @ w2


def nsa_mixer(q, k_cmp, v_cmp, k_slc, v_slc, k_win, v_win, gates, cmp_pe, cmp_w1, cmp_w2):
    B, T = q.shape[:2]
    G, R, dh = NSA_KV_HEADS, NSA_GROUP, NSA_HEAD_DIM
    scale = dh ** -0.5
    slopes = alibi_slopes(NSA_HEADS).reshape(G, R)

    kc = compress(k_cmp, cmp_pe[0], cmp_w1[0], cmp_w2[0])
    vc = compress(v_cmp, cmp_pe[1], cmp_w1[1], cmp_w2[1])
    nc = kc.shape[1]
    nsel = T // SLC_LEN
    n_top = min(SLC_TOPN, nsel)

    c_start_np = np.arange(nc) * CMP_STRIDE
    c_end_np = c_start_np + CMP_LEN - 1
    s_start_np = np.arange(nsel) * SLC_LEN
    s_end_np = s_start_np + SLC_LEN - 1
    overlap = jnp.asarray(np.clip(np.minimum(c_end_np[:, None], s_end_np[None, :])
                                  - np.maximum(c_start_np[:, None], s_start_np[None, :]) + 1,
                                  0, None) / CMP_LEN, dtype=jnp.float32)
    c_end = jnp.asarray(c_end_np, dtype=jnp.int32)
    c_mid = jnp.asarray(c_start_np + (CMP_LEN - 1) / 2.0, dtype=jnp.float32)

    k_blocks = k_slc.reshape(B, nsel, SLC_LEN, G, dh).transpose(0, 3, 1, 2, 4)
    v_blocks = v_slc.reshape(B, nsel, SLC_LEN, G, dh).transpose(0, 3, 1, 2, 4)
    k_win_p = jnp.pad(k_win, ((0, 0), (WINDOW, 0), (0, 0), (0, 0)))
    v_win_p = jnp.pad(v_win, ((0, 0), (WINDOW, 0), (0, 0), (0, 0)))

    nb = T // Q_BLOCK
    qb = q.reshape(B, nb, Q_BLOCK, G, R, dh).transpose(1, 0, 2, 3, 4, 5)
    gb = gates.reshape(B, nb, Q_BLOCK, G, R, 3).transpose(1, 0, 2, 3, 4, 5)
    b_ix = jnp.arange(B)[:, None, None, None]
    g_ix = jnp.arange(G)[None, None, :, None]
    jb = jnp.arange(nsel)

    def block_fn(args):
        i, qi, gi = args
        t = i * Q_BLOCK + jnp.arange(Q_BLOCK)
        tf = t.astype(jnp.float32)

        s_c = jnp.einsum('bqgrd,bngd->bgrqn', qi, kc).astype(jnp.float32) * scale
        s_c = s_c - slopes[None, :, :, None, None] * (tf[:, None] - c_mid[None, :])
        p_c = masked_softmax(s_c, (c_end[None, :] <= t[:, None])[None, None, None])
        o_c = jnp.einsum('bgrqn,bngd->bqgrd', p_c.astype(vc.dtype), vc)

        imp = jnp.einsum('bgrqn,nj->bqgj', p_c, overlap)
        cur = t // SLC_LEN
        forced = (jb[None, :] == 0) | (jb[None, :] == cur[:, None]) | (jb[None, :] == cur[:, None] - 1)
        valid = jb[None, :] <= cur[:, None]
        imp = jnp.where(forced[None, :, None, :], 1e4,
                        jnp.where(valid[None, :, None, :], imp, -1.0))
        _, idx = lax.top_k(imp, n_top)

        ks = k_blocks[b_ix, g_ix, idx].reshape(B, Q_BLOCK, G, n_top * SLC_LEN, dh)
        vs = v_blocks[b_ix, g_ix, idx].reshape(B, Q_BLOCK, G, n_top * SLC_LEN, dh)
        spos = (idx[..., None] * SLC_LEN + jnp.arange(SLC_LEN)).reshape(B, Q_BLOCK, G, 1, n_top * SLC_LEN)
        dist_s = t[None, :, None, None, None] - spos
        s_s = jnp.einsum('bqgrd,bqgmd->bqgrm', qi, ks).astype(jnp.float32) * scale
        s_s = s_s - slopes[None, None, :, :, None] * dist_s.astype(jnp.float32)
        p_s = masked_softmax(s_s, dist_s >= 0)
        o_s = jnp.einsum('bqgrm,bqgmd->bqgrd', p_s.astype(vs.dtype), vs)

        kw = lax.dynamic_slice_in_dim(k_win_p, i * Q_BLOCK, WINDOW + Q_BLOCK, axis=1)
        vw = lax.dynamic_slice_in_dim(v_win_p, i * Q_BLOCK, WINDOW + Q_BLOCK, axis=1)
        kpos = i * Q_BLOCK - WINDOW + jnp.arange(WINDOW + Q_BLOCK)
        dist_w = t[:, None] - kpos[None, :]
        mask_w = (dist_w >= 0) & (dist_w < WINDOW) & (kpos[None, :] >= 0)
        s_w = jnp.einsum('bqgrd,bkgd->bqgrk', qi, kw).astype(jnp.float32) * scale
        s_w = s_w - slopes[None, None, :, :, None] * dist_w[None, :, None, None, :].astype(jnp.float32)
        p_w = masked_softmax(s_w, mask_w[None, :, None, None, :])
        o_w = jnp.einsum('bqgrk,bkgd->bqgrd', p_w.astype(vw.dtype), vw)

        gs = jax.nn.sigmoid(gi.astype(jnp.float32))
        o = (gs[..., 0:1] * o_c.astype(jnp.float32) + gs[..., 1:2] * o_s.astype(jnp.float32)
             + gs[..., 2:3] * o_w.astype(jnp.float32))
        return o.reshape(B, Q_BLOCK, NSA_WIDTH).astype(q.dtype)

    out = lax.map(block_fn, (jnp.arange(nb), qb, gb))
    return out.transpose(1, 0, 2, 3).reshape(B, T, NSA_WIDTH)


def retention(q, k, v, g, gn_gain):
    B, T, Hr, d = q.shape
    C = RET_CHUNK
    nch = T // C
    log_g = jnp.log(1.0 - jnp.exp2(-5.0 - jnp.arange(Hr, dtype=jnp.float32)))
    pos = jnp.arange(C, dtype=jnp.float32)
    diff = pos[:, None] - pos[None, :]
    decay_mask = jnp.where(diff >= 0, jnp.exp(jnp.maximum(diff, 0.0) * log_g[:, None, None]), 0.0)
    q_dec = jnp.exp((pos + 1.0) * log_g[:, None])
    k_dec = jnp.exp((C - 1.0 - pos) * log_g[:, None])
    chunk_dec = jnp.exp(C * log_g)

    def to_chunks(a):
        return a.astype(jnp.float32).reshape(B, nch, C, Hr, d).transpose(1, 0, 3, 2, 4)
    qc, kc, vc = to_chunks(q), to_chunks(k * (d ** -0.5)), to_chunks(v)

    def step(state, xs):
        qi, ki, vi = xs
        s = jnp.einsum('bhnd,bhmd->bhnm', qi, ki) * decay_mask[None]
        inner = jnp.einsum('bhnm,bhmd->bhnd', s, vi)
        cross = jnp.einsum('bhnd,bhde->bhne', qi, state) * q_dec[None, :, :, None]
        new_state = (state * chunk_dec[None, :, None, None]
                     + jnp.einsum('bhmd,bhme->bhde', ki * k_dec[None, :, :, None], vi))
        return new_state, inner + cross

    state0 = jnp.zeros((B, Hr, d, d), jnp.float32)
    _, o = lax.scan(step, state0, (qc, kc, vc))
    o = o.transpose(1, 0, 3, 2, 4).reshape(B, T, Hr, d)
    mu = jnp.mean(o, axis=-1, keepdims=True)
    var = jnp.mean(jnp.square(o - mu), axis=-1, keepdims=True)
    o = (o - mu) * lax.rsqrt(var + EPS) * gn_gain.astype(jnp.float32).reshape(Hr, d)
    o = o.reshape(B, T, RET_WIDTH) * jax.nn.silu(g.astype(jnp.float32))
    return o.astype(q.dtype)


def hybrid_mixer(h, w_in, cmp_pe, cmp_w1, cmp_w2, nsa_out_g, ret_gn_g, w_out):
    B, T, _ = h.shape
    parts = jnp.split(h @ w_in, IN_OFFSETS, axis=-1)
    q_n, kc, vc, ks, vs, kw, vw, gates, q_r, k_r, v_r, g_r = parts
    kvs = lambda a: a.reshape(B, T, NSA_KV_HEADS, NSA_HEAD_DIM)
    o_nsa = nsa_mixer(q_n.reshape(B, T, NSA_HEADS, NSA_HEAD_DIM), kvs(kc), kvs(vc), kvs(ks), kvs(vs),
                      kvs(kw), kvs(vw), gates.reshape(B, T, NSA_HEADS, 3), cmp_pe, cmp_w1, cmp_w2)
    o_nsa = rms_norm(o_nsa, nsa_out_g)
    rs = lambda a: a.reshape(B, T, RET_HEADS, RET_HEAD_DIM)
    o_ret = retention(rs(q_r), rs(k_r), rs(v_r), g_r, ret_gn_g)
    return jnp.concatenate([o_nsa, o_ret], axis=-1) @ w_out


def memory_xattn(h, m, wq, wk, wv, wo):
    B, T, _ = h.shape
    M = m.shape[1]
    q = (h @ wq).reshape(B, T, MEM_HEADS, MEM_HEAD_DIM)
    k = (m @ wk).reshape(B, M, MEM_HEADS, MEM_HEAD_DIM)
    v = (m @ wv).reshape(B, M, MEM_HEADS, MEM_HEAD_DIM)
    s = jnp.einsum('bthd,bmhd->bhtm', q, k).astype(jnp.float32) * (MEM_HEAD_DIM ** -0.5)
    p = jax.nn.softmax(s, axis=-1)
    o = jnp.einsum('bhtm,bmhd->bthd', p.astype(v.dtype), v).reshape(B, T, D_MODEL)
    return o @ wo


def setup_inputs(seed: int = 0) -> dict:
    key = jax.random.key(seed)
    ks = jax.random.split(key, 28)
    f32 = jnp.float32

    def dense(k, shape, fan_in):
        return jax.random.normal(k, shape, f32) * (fan_in ** -0.5)

    def gain(k, n):
        return 1.0 + 0.02 * jax.random.normal(k, (DEPTH, n), f32)

    L, dh = CMP_LEN, NSA_HEAD_DIM
    return {
        "x": jax.random.normal(ks[0], (BATCH, SEQ, D_MODEL), f32),
        "mem": jax.random.normal(ks[1], (BATCH, MEM_LEN, D_MODEL), f32),
        "ffn1_pre_g": gain(ks[2], D_MODEL),
        "ffn1_w_gate": dense(ks[3], (DEPTH, D_MODEL, D_FF), D_MODEL),
        "ffn1_w_up": dense(ks[4], (DEPTH, D_MODEL, D_FF), D_MODEL),
        "ffn1_w_down": dense(ks[5], (DEPTH, D_FF, D_MODEL), D_FF),
        "ffn1_post_g": gain(ks[6], D_MODEL),
        "mix_pre_g": gain(ks[7], D_MODEL),
        "w_in": dense(ks[8], (DEPTH, D_MODEL, IN_WIDTH), D_MODEL),
        "cmp_pe": 0.5 * jax.random.normal(ks[9], (DEPTH, 2, L, dh), f32),
        "cmp_w1": dense(ks[10], (DEPTH, 2, L * dh, CMP_HIDDEN), L * dh),
        "cmp_w2": dense(ks[11], (DEPTH, 2, CMP_HIDDEN, dh), CMP_HIDDEN),
        "nsa_out_g": gain(ks[12], NSA_WIDTH),
        "ret_gn_g": gain(ks[13], RET_WIDTH),
        "w_out": dense(ks[14], (DEPTH, NSA_WIDTH + RET_WIDTH, D_MODEL), NSA_WIDTH + RET_WIDTH),
        "mix_post_g": gain(ks[15], D_MODEL),
        "xa_pre_g": gain(ks[16], D_MODEL),
        "xa_mem_g": gain(ks[17], D_MODEL),
        "xa_wq": dense(ks[18], (DEPTH, D_MODEL, D_MODEL), D_MODEL),
        "xa_wk": dense(ks[19], (DEPTH, D_MODEL, D_MODEL), D_MODEL),
        "xa_wv": dense(ks[20], (DEPTH, D_MODEL, D_MODEL), D_MODEL),
        "xa_wo": dense(ks[21], (DEPTH, D_MODEL, D_MODEL), D_MODEL),
        "xa_post_g": gain(ks[22], D_MODEL),
        "ffn2_pre_g": gain(ks[23], D_MODEL),
        "ffn2_w_gate": dense(ks[24], (DEPTH, D_MODEL, D_FF), D_MODEL),
        "ffn2_w_up": dense(ks[25], (DEPTH, D_MODEL, D_FF), D_MODEL),
        "ffn2_w_down": dense(ks[26], (DEPTH, D_FF, D_MODEL), D_FF),
        "ffn2_post_g": gain(ks[27], D_MODEL),
    }


def reference(x, mem, ffn1_pre_g, ffn1_w_gate, ffn1_w_up, ffn1_w_down, ffn1_post_g,
              mix_pre_g, w_in, cmp_pe, cmp_w1, cmp_w2, nsa_out_g, ret_gn_g, w_out, mix_post_g,
              xa_pre_g, xa_mem_g, xa_wq, xa_wk, xa_wv, xa_wo, xa_post_g,
              ffn2_pre_g, ffn2_w_gate, ffn2_w_up, ffn2_w_down, ffn2_post_g):
    for l in range(DEPTH):
        y = swiglu(rms_norm(x, ffn1_pre_g[l]), ffn1_w_gate[l], ffn1_w_up[l], ffn1_w_down[l])
        x = x + 0.5 * rms_norm(y, ffn1_post_g[l])
        y = hybrid_mixer(rms_norm(x, mix_pre_g[l]), w_in[l], cmp_pe[l], cmp_w1[l], cmp_w2[l],
                         nsa_out_g[l], ret_gn_g[l], w_out[l])
        x = x + rms_norm(y, mix_post_g[l])
        y = memory_xattn(rms_norm(x, xa_pre_g[l]), rms_norm(mem, xa_mem_g[l]),
                         xa_wq[l], xa_wk[l], xa_wv[l], xa_wo[l])
        x = x + rms_norm(y, xa_post_g[l])
        y = swiglu(rms_norm(x, ffn2_pre_g[l]), ffn2_w_gate[l], ffn2_w_up[l], ffn2_w_down[l])
        x = x + 0.5 * rms_norm(y, ffn2_post_g[l])
    return x
```

```python
import numpy as np
import concourse.bass as bass
import concourse.mybir as mybir
from concourse.bass_utils import run_bass_kernel_spmd
from contextlib import ExitStack

F32 = mybir.dt.float32
BF16 = mybir.dt.bfloat16
AF = mybir.ActivationFunctionType
ALU = mybir.AluOpType
AX = mybir.AxisListType

D = 1024
DFF = 2816
NF = DFF // 128
INW = 3352
EPS = 1e-6
NCORES = 8


class T:
    def __init__(self, ap=None):
        self.ap = ap
        self.w = None
        self.r = {}

    def __getitem__(self, k):
        return self.ap[k]


class Sched:
    NDS = 8

    def __init__(self, nc, es):
        self.nc = nc
        self.es = es
        self.keys = ['pe', 'act', 'dve', 'pool', 'sp']
        self.sem = {}
        for k in self.keys[:4]:
            self.sem[k] = es.enter_context(nc.semaphore("s_" + k))
        self.cnt = {k: 0 for k in self.keys}
        self.seen = {k: {} for k in self.keys}
        self.prog = {k: [] for k in self.keys}
        self.dq = {}
        for q in ['sp', 'pool', 'act']:
            sems = []
            for i in range(self.NDS):
                key = "d_%s%d" % (q, i)
                self.sem[key] = es.enter_context(nc.semaphore(key))
                sems.append(key)
            self.dq[q] = [sems, 0]
        self.n_ins = 0
        self.uid = 0

    def tile(self, shape, dtype, name, psum=False):
        self.uid += 1
        name = "%s_%d" % (name, self.uid)
        if psum:
            esz = 4 if dtype == F32 else 2
            n = 1
            for s_ in shape[1:]:
                n *= s_
            per_bank = 2048 // esz
            npad = ((n + per_bank - 1) // per_bank) * per_bank
            raw = self.es.enter_context(self.nc.psum_tensor(name, [shape[0], npad], dtype))
            ap = raw[:, 0:n]
            if len(shape) == 3:
                ap = ap.rearrange("p (a b) -> p a b", a=shape[1])
            elif len(shape) == 4:
                ap = ap.rearrange("p (a b c) -> p a b c", a=shape[1], b=shape[2])
            return T(ap)
        else:
            t = self.es.enter_context(self.nc.sbuf_tensor(name, list(shape), dtype))
        return T(t)

    def _need(self, e, deps):
        for (k, v) in deps:
            if self.seen[e].get(k, 0) >= v:
                continue
            if k == e and e == 'pe':
                continue
            self.prog[e].append(('wait', k, v))
            self.seen[e][k] = v

    def _deps(self, reads, writes):
        deps = []
        for t in reads:
            if t.w:
                deps.append(t.w)
        for t in writes:
            if t.w:
                deps.append(t.w)
            deps += list(t.r.items())
        return deps

    def _mark(self, ev, reads, writes):
        for t in reads:
            if t.r.get(ev[0], 0) < ev[1]:
                t.r[ev[0]] = ev[1]
        for t in writes:
            t.w = ev
            t.r = {}

    def op(self, e, fn, reads=(), writes=()):
        self._need(e, self._deps(reads, writes))
        self.cnt[e] += 1
        ev = (e, self.cnt[e])
        self.prog[e].append(('ins', fn, e, 1))
        self._mark(ev, reads, writes)
        self.n_ins += 1
        return ev

    def dma(self, q, fn, reads=(), writes=()):
        sems, n = self.dq[q]
        key = sems[n % self.NDS]
        val = 16 * (n // self.NDS + 1)
        deps = self._deps(reads, writes)
        if n >= self.NDS:
            deps.append((key, val - 16))
        self._need(q, deps)
        self.dq[q][1] = n + 1
        ev = (key, val)
        self.prog[q].append(('ins', fn, key, 16))
        self._mark(ev, reads, writes)
        self.n_ins += 1
        return ev

    def _all_events(self):
        deps = []
        for q, (sems, n) in self.dq.items():
            for i in range(min(n, self.NDS)):
                last = n - 1 - ((n - 1 - i) % self.NDS)
                deps.append((sems[i], 16 * (last // self.NDS + 1)))
        for k in self.keys[:4]:
            if self.cnt[k]:
                deps.append((k, self.cnt[k]))
        return deps

    def barrier(self):
        deps = self._all_events()
        for e in self.keys:
            self._need(e, deps)

    def finish(self):
        self._need('sp', self._all_events())

    def emit(self):
        S = self

        def replay(k, e):
            for it in S.prog[k]:
                if it[0] == 'wait':
                    e.wait_ge(S.sem[it[1]], it[2])
                else:
                    ins = it[1](e)
                    ins.then_inc(S.sem[it[2]], it[3])

        with self.nc.Block() as block:
            @block.tensor
            def _(e):
                replay('pe', e)

            @block.scalar
            def _(e):
                replay('act', e)

            @block.vector
            def _(e):
                replay('dve', e)

            @block.gpsimd
            def _(e):
                replay('pool', e)

            @block.sync
            def _(e):
                replay('sp', e)


def load_bc(S, dram_row, width, name, q='sp'):
    t = S.tile([128, width], F32, name)
    S.dma(q, lambda e, t=t: e.dma_start(out=t[:], in_=dram_row.partition_broadcast(128)), writes=[t])
    return t


def load_ident(S, ident_d, pfx):
    ident_f = S.tile([128, 128], F32, pfx + "idf")
    ident = S.tile([128, 128], BF16, pfx + "idb")
    S.dma('sp', lambda e: e.dma_start(out=ident_f[:], in_=ident_d), writes=[ident_f])
    S.op('dve', lambda e: e.tensor_copy(out=ident[:], in_=ident_f[:]), reads=[ident_f], writes=[ident])
    return ident


def rms_stats(S, src_ap, src_t, width, scr, ss, rstd):
    S.op('act', lambda e: e.activation(out=scr[:, 0:width], in_=src_ap, func=AF.Square, accum_out=ss[:]),
         reads=[src_t], writes=[scr, ss])
    S.op('act', lambda e: e.activation(out=rstd[:], in_=ss[:], func=AF.Sqrt, bias=EPS, scale=1.0 / width),
         reads=[ss], writes=[rstd])
    S.op('dve', lambda e: e.reciprocal(out=rstd[:], in_=rstd[:]), reads=[rstd], writes=[rstd])


class NormT:
    def __init__(self, S, ident, pfx):
        self.S = S
        self.ident = ident
        self.scr = S.tile([128, D], F32, pfx + "scr")
        self.ss = S.tile([128, 1], F32, pfx + "ss")
        self.rstd = S.tile([128, 1], F32, pfx + "rstd")
        self.xn = S.tile([128, D], BF16, pfx + "xn")
        self.pt = S.tile([128, 8, 128], BF16, pfx + "pt", psum=True)

    def norm(self, src_ap, src_t, g_bc, out_ap, out_t, width=D):
        S = self.S
        rms_stats(S, src_ap, src_t, width, self.scr, self.ss, self.rstd)
        S.op('dve', lambda e: e.scalar_tensor_tensor(out=out_ap, in0=src_ap, scalar=self.rstd[:], in1=g_bc[:, 0:width],
                                                     op0=ALU.mult, op1=ALU.mult),
             reads=[src_t, self.rstd, g_bc], writes=[out_t])

    def transp(self, src_t, dstT, col0, nchunk=8):
        S = self.S
        for c in range(nchunk):
            S.op('pe', lambda e, c=c: e.transpose(out=self.pt[:, c, :], in_=src_t[:, c * 128:(c + 1) * 128],
                                                  identity=self.ident[:]), reads=[src_t, self.ident], writes=[self.pt])
        S.op('act', lambda e: e.activation(out=dstT[:, 0:nchunk, col0:col0 + 128], in_=self.pt[:, 0:nchunk, :],
                                           func=AF.Copy), reads=[self.pt], writes=[dstT])

    def norm_T(self, src_ap, src_t, g_bc, dstT, col0):
        self.norm(src_ap, src_t, g_bc, self.xn[:], self.xn)
        self.transp(self.xn, dstT, col0)


def load_w_bf16(S, w_d, kc, ncol, name):
    t = S.tile([128, kc, ncol], BF16, name)
    v = w_d.rearrange("(c p) f -> p c f", p=128)
    for c in range(kc):
        S.dma('pool', lambda e, c=c: e.dma_start(out=t[:, c, :], in_=v[:, c, :]), writes=[t])
    return t


def phase_ffn(S, x, y, wg, wu, wd, gpre, gpost, ident_d, ntok, pfx):
    GT = 256
    ng = ntok // GT
    wg_sb = load_w_bf16(S, wg, 8, DFF, pfx + "wg")
    wu_sb = load_w_bf16(S, wu, 8, DFF, pfx + "wu")
    wd_sb = load_w_bf16(S, wd, NF, D, pfx + "wd")
    gpre_bc = load_bc(S, gpre, D, pfx + "gpre")
    gpost_bc = load_bc(S, gpost, D, pfx + "gpost")
    S.op('dve', lambda e: e.tensor_scalar_mul(out=gpost_bc[:], in0=gpost_bc[:], scalar1=0.5),
         reads=[gpost_bc], writes=[gpost_bc])
    ident = load_ident(S, ident_d, pfx)
    NT = NormT(S, ident, pfx)
    xs = [S.tile([128, 2, D], F32, pfx + "x%d" % i) for i in range(2)]
    xnT = S.tile([128, 8, GT], BF16, pfx + "xnT")
    hT = [S.tile([128, GT], BF16, pfx + "hT%d" % i) for i in range(2)]
    sg = [S.tile([128, GT], F32, pfx + "sg%d" % i) for i in range(2)]
    ysb = S.tile([128, D], F32, pfx + "ysb")
    osb = [S.tile([128, D], F32, pfx + "osb%d" % i) for i in range(2)]
    pgu = [S.tile([128, 512], F32, pfx + "pgu%d" % i, psum=True) for i in range(2)]
    acc = [[S.tile([128, 512], F32, pfx + "acc%d%d" % (b, h), psum=True) for h in range(2)] for b in range(2)]
    xv = x.rearrange("(n p) d -> p n d", p=128)
    yv = y.rearrange("(n p) d -> p n d", p=128)
    for g in range(ng):
        xg = xs[g % 2]
        S.dma('sp', lambda e, g=g, xg=xg: e.dma_start(out=xg[:], in_=xv[:, 2 * g:2 * g + 2, :]), writes=[xg])
        for b in range(2):
            NT.norm_T(xg[:, b, :], xg, gpre_bc, xnT, b * 128)

        def down(f):
            h = hT[f % 2]
            for b in range(2):
                for hf in range(2):
                    S.op('pe', lambda e, b=b, hf=hf, h=h, f=f: e.matmul(
                        acc[b][hf][:], lhsT=h[:, b * 128:(b + 1) * 128],
                        rhs=wd_sb[:, f, hf * 512:(hf + 1) * 512], start=(f == 0), stop=(f == NF - 1)),
                        reads=[h, wd_sb], writes=[acc[b][hf]])

        for f in range(NF):
            p = pgu[f % 2]
            for c in range(8):
                S.op('pe', lambda e, c=c, f=f, p=p: e.matmul(
                    p[:, 0:GT], lhsT=wg_sb[:, c, f * 128:(f + 1) * 128], rhs=xnT[:, c, :],
                    start=(c == 0), stop=(c == 7)), reads=[wg_sb, xnT], writes=[p])
            for c in range(8):
                S.op('pe', lambda e, c=c, f=f, p=p: e.matmul(
                    p[:, GT:2 * GT], lhsT=wu_sb[:, c, f * 128:(f + 1) * 128], rhs=xnT[:, c, :],
                    start=(c == 0), stop=(c == 7)), reads=[wu_sb, xnT], writes=[p])
            if f >= 1:
                down(f - 1)
            s_ = sg[f % 2]
            h = hT[f % 2]
            S.op('act', lambda e, p=p, s_=s_: e.activation(out=s_[:], in_=p[:, 0:GT], func=AF.Silu),
                 reads=[p], writes=[s_])
            S.op('dve', lambda e, p=p, s_=s_, h=h: e.tensor_tensor(out=h[:], in0=s_[:], in1=p[:, GT:2 * GT],
                                                                   op=ALU.mult), reads=[p, s_], writes=[h])
        down(NF - 1)
        for b in range(2):
            for hf in range(2):
                S.op('act', lambda e, b=b, hf=hf: e.activation(out=ysb[:, hf * 512:(hf + 1) * 512],
                                                               in_=acc[b][hf][:], func=AF.Copy),
                     reads=[acc[b][hf]], writes=[ysb])
            o = osb[b]
            NT.norm(ysb[:], ysb, gpost_bc, o[:], o)
            S.op('pool', lambda e, o=o, b=b, xg=xg: e.tensor_tensor(out=o[:], in0=o[:], in1=xg[:, b, :], op=ALU.add),
                 reads=[o, xg], writes=[o])
            S.dma('sp', lambda e, o=o, g=g, b=b: e.dma_start(out=yv[:, 2 * g + b, :], in_=o[:]), reads=[o])


def phase_proj(S, x, proj, w_in, g, ident_d, ntok, pfx):
    w_sb = load_w_bf16(S, w_in, 8, INW, pfx + "win")
    g_bc = load_bc(S, g, D, pfx + "g")
    ident = load_ident(S, ident_d, pfx)
    NT = NormT(S, ident, pfx)
    xs = [S.tile([128, D], F32, pfx + "x%d" % i) for i in range(2)]
    xnT = S.tile([128, 8, 128], BF16, pfx + "xnT")
    osb = [S.tile([128, INW], F32, pfx + "o%d" % i) for i in range(2)]
    pp = [S.tile([128, 512], F32, pfx + "pp%d" % i, psum=True) for i in range(2)]
    n = 0
    for b in range(ntok // 128):
        xb = xs[b % 2]
        S.dma('sp', lambda e, b=b, xb=xb: e.dma_start(out=xb[:], in_=x[b * 128:(b + 1) * 128, :]), writes=[xb])
        NT.norm_T(xb[:], xb, g_bc, xnT, 0)
        o = osb[b % 2]
        for cg in range(7):
            w = min(512, INW - 512 * cg)
            p = pp[n % 2]
            for c in range(8):
                S.op('pe', lambda e, c=c, cg=cg, w=w, p=p: e.matmul(
                    p[:, 0:w], lhsT=xnT[:, c, :], rhs=w_sb[:, c, cg * 512:cg * 512 + w],
                    start=(c == 0), stop=(c == 7)), reads=[xnT, w_sb], writes=[p])
            if n % 2 == 0:
                S.op('act', lambda e, cg=cg, w=w, p=p, o=o: e.activation(out=o[:, cg * 512:cg * 512 + w], in_=p[:, 0:w],
                                                                        func=AF.Copy), reads=[p], writes=[o])
            else:
                S.op('dve', lambda e, cg=cg, w=w, p=p, o=o: e.tensor_copy(out=o[:, cg * 512:cg * 512 + w], in_=p[:, 0:w]),
                     reads=[p], writes=[o])
            n += 1
        S.dma('sp', lambda e, b=b, o=o: e.dma_start(out=proj[b * 128:(b + 1) * 128, :], in_=o[:]), reads=[o])


def ret_gammas():
    return [1.0 - 2.0 ** (-5.0 - h) for h in range(4)]


def phase_ret(S, d, mix, Tn, pfx):
    NB = Tn // 128
    dmT = S.tile([128, 4, 128], F32, pfx + "dmT")
    qdecb = S.tile([128, 4, 128], F32, pfx + "qdecb")
    kdec = S.tile([128, 4], F32, pfx + "kdec")
    S.dma('sp', lambda e: e.dma_start(out=dmT[:], in_=d["c_dmT"]), writes=[dmT])
    S.dma('sp', lambda e: e.dma_start(out=qdecb[:], in_=d["c_qdecb"]), writes=[qdecb])
    S.dma('sp', lambda e: e.dma_start(out=kdec[:], in_=d["c_kdec"]), writes=[kdec])
    gn_bc = load_bc(S, d["ret_gn_g"], 512, pfx + "gn")
    st = S.tile([128, 4, 128], F32, pfx + "st")
    st_bf = S.tile([128, 4, 128], BF16, pfx + "stb")
    S.op('pool', lambda e: e.memset(st[:], 0.0), writes=[st])
    kbuf = [S.tile([128, 512], F32, pfx + "kb%d" % i) for i in range(2)]
    vbuf = [S.tile([128, 512], BF16, pfx + "vb%d" % i) for i in range(2)]
    kd = [S.tile([128, 4, 128], BF16, pfx + "kd%d" % i) for i in range(2)]
    qT = S.tile([128, 4, 128], BF16, pfx + "qT")
    kT = S.tile([128, 4, 128], BF16, pfx + "kT")
    qTd = S.tile([128, 4, 128], BF16, pfx + "qTd")
    sTm = S.tile([128, 4, 128], BF16, pfx + "sTm")
    gt = S.tile([128, 512], F32, pfx + "gt")
    sgl = S.tile([128, 512], F32, pfx + "sgl")
    osb = S.tile([128, 4, 128], F32, pfx + "osb")
    sq = S.tile([128, 4, 128], F32, pfx + "sq")
    oret = S.tile([128, 4, 128], F32, pfx + "oret")
    s1 = S.tile([128, 4], F32, pfx + "s1")
    s2 = S.tile([128, 4], F32, pfx + "s2")
    mean = S.tile([128, 4], F32, pfx + "mean")
    msq = S.tile([128, 4], F32, pfx + "msq")
    var = S.tile([128, 4], F32, pfx + "var")
    psT = S.tile([128, 4, 128], F32, pfx + "psT", psum=True)
    po = S.tile([128, 4, 128], F32, pfx + "po", psum=True)
    pu = [S.tile([128, 4, 128], F32, pfx + "pu%d" % i, psum=True) for i in range(2)]
    gam = ret_gammas()
    rsel = S.tile([128, 4], F32, pfx + "rsel")
    S.dma('sp', lambda e: e.dma_start(out=rsel[:], in_=d["c_rsel"]), writes=[rsel])
    st_sel = S.tile([128, 4, 128], F32, pfx + "stsel")
    vown = S.tile([128, 512], BF16, pfx + "vown")
    for j in range(NB):
        m = j // 4
        jj = j % 4
        kt = kbuf[j % 2]
        vt = vbuf[j % 2]
        S.dma('sp', lambda e, j=j, kt=kt: e.dma_start(out=kt[:], in_=d["kr_all"][:, j, :]), writes=[kt])
        S.dma('pool', lambda e, j=j, vt=vt: e.dma_start(out=vt[:], in_=d["vr_all"][:, j, :]), writes=[vt])
        if jj == 0:
            S.op('dve', lambda e: e.tensor_scalar_mul(out=st_sel[:], in0=st[:], scalar1=rsel[:, 0:1]),
                 reads=[st, rsel], writes=[st_sel])
        else:
            S.op('dve', lambda e, jj=jj: e.scalar_tensor_tensor(out=st_sel[:], in0=st[:], scalar=rsel[:, jj:jj + 1], in1=st_sel[:],
                                                               op0=ALU.mult, op1=ALU.add), reads=[st, rsel, st_sel], writes=[st_sel])
        if jj == 3:
            S.op('act', lambda e: e.activation(out=st_bf[:], in_=st_sel[:], func=AF.Copy), reads=[st_sel], writes=[st_bf])
            S.dma('pool', lambda e, m=m: e.dma_start(out=qT[:], in_=d["qrT"][m]), writes=[qT])
            S.dma('pool', lambda e, m=m: e.dma_start(out=kT[:], in_=d["krT"][m]), writes=[kT])
            S.dma('pool', lambda e, m=m: e.dma_start(out=vown[:], in_=d["vr_own"][m]), writes=[vown])
            S.dma('sp', lambda e, m=m: e.dma_start(out=gt[:], in_=d["gr"][m]), writes=[gt])
            S.op('dve', lambda e: e.tensor_tensor(out=qTd[:], in0=qT[:], in1=qdecb[:], op=ALU.mult),
                 reads=[qT, qdecb], writes=[qTd])
            for h in range(4):
                S.op('pe', lambda e, h=h: e.matmul(psT[:, h, :], lhsT=kT[:, h, :], rhs=qT[:, h, :], start=True, stop=True),
                     reads=[kT, qT], writes=[psT])
            S.op('dve', lambda e: e.tensor_tensor(out=sTm[:], in0=psT[:], in1=dmT[:], op=ALU.mult),
                 reads=[psT, dmT], writes=[sTm])
            for h in range(4):
                S.op('pe', lambda e, h=h: e.matmul(po[:, h, :], lhsT=sTm[:, h, :], rhs=vown[:, h * 128:(h + 1) * 128],
                                                   start=True, stop=False), reads=[sTm, vown], writes=[po])
                S.op('pe', lambda e, h=h: e.matmul(po[:, h, :], lhsT=qTd[:, h, :], rhs=st_bf[:, h, :],
                                                   start=False, stop=True), reads=[qTd, st_bf], writes=[po])
            S.op('act', lambda e: e.activation(out=osb[:], in_=po[:], func=AF.Copy), reads=[po], writes=[osb])
            S.op('dve', lambda e: e.tensor_reduce(out=s1[:], in_=osb[:], axis=AX.X, op=ALU.add), reads=[osb], writes=[s1])
            S.op('act', lambda e: e.activation(out=sq[:], in_=osb[:], func=AF.Square), reads=[osb], writes=[sq])
            S.op('dve', lambda e: e.tensor_reduce(out=s2[:], in_=sq[:], axis=AX.X, op=ALU.add), reads=[sq], writes=[s2])
            S.op('dve', lambda e: e.tensor_scalar_mul(out=mean[:], in0=s1[:], scalar1=1.0 / 128), reads=[s1], writes=[mean])
            S.op('dve', lambda e: e.tensor_tensor(out=msq[:], in0=mean[:], in1=mean[:], op=ALU.mult), reads=[mean], writes=[msq])
            S.op('dve', lambda e: e.scalar_tensor_tensor(out=var[:], in0=s2[:], scalar=1.0 / 128, in1=msq[:],
                                                         op0=ALU.mult, op1=ALU.subtract), reads=[s2, msq], writes=[var])
            S.op('act', lambda e: e.activation(out=var[:], in_=var[:], func=AF.Sqrt, bias=EPS, scale=1.0),
                 reads=[var], writes=[var])
            S.op('dve', lambda e: e.reciprocal(out=var[:], in_=var[:]), reads=[var], writes=[var])
            S.op('dve', lambda e: e.tensor_tensor(out=osb[:], in0=osb[:], in1=mean[:].unsqueeze(2).to_broadcast([128, 4, 128]),
                                                  op=ALU.subtract), reads=[osb, mean], writes=[osb])
            S.op('dve', lambda e: e.tensor_tensor(out=osb[:], in0=osb[:], in1=var[:].unsqueeze(2).to_broadcast([128, 4, 128]),
                                                  op=ALU.mult), reads=[osb, var], writes=[osb])
            S.op('dve', lambda e: e.tensor_tensor(out=osb[:], in0=osb[:], in1=gn_bc[:].rearrange("p (h d) -> p h d", h=4),
                                                  op=ALU.mult), reads=[osb, gn_bc], writes=[osb])
            S.op('act', lambda e: e.activation(out=sgl[:], in_=gt[:], func=AF.Silu), reads=[gt], writes=[sgl])
            S.op('dve', lambda e: e.tensor_tensor(out=oret[:], in0=osb[:], in1=sgl[:].rearrange("p (h d) -> p h d", h=4),
                                                  op=ALU.mult), reads=[osb, sgl], writes=[oret])
            S.dma('sp', lambda e, m=m: e.dma_start(out=mix[m * 128:(m + 1) * 128, 512:1024],
                                                   in_=oret[:].rearrange("p h d -> p (h d)")), reads=[oret])
        if j == NB - 1:
            break
        kdj = kd[j % 2]
        puj = pu[j % 2]
        S.op('dve', lambda e, kt=kt, kdj=kdj: e.tensor_tensor(
            out=kdj[:], in0=kt[:].rearrange("p (h d) -> p h d", h=4),
            in1=kdec[:].unsqueeze(2).to_broadcast([128, 4, 128]), op=ALU.mult), reads=[kt, kdec], writes=[kdj])
        for h in range(4):
            S.op('pe', lambda e, h=h, kdj=kdj, vt=vt, puj=puj: e.matmul(
                puj[:, h, :], lhsT=kdj[:, h, :], rhs=vt[:, h * 128:(h + 1) * 128], start=True, stop=True),
                reads=[kdj, vt], writes=[puj])
        for h in range(4):
            S.op('dve', lambda e, h=h, puj=puj: e.scalar_tensor_tensor(
                out=st[:, h, :], in0=st[:, h, :], scalar=float(gam[h] ** 128), in1=puj[:, h, :],
                op0=ALU.mult, op1=ALU.add), reads=[st, puj], writes=[st])


def phase_cmp(S, d, Tn, kcT, vc_aug, pfx):
    NCp = Tn // 16
    Nc = NCp - 1
    with ExitStack() as es2:
        old = S.es
        S.es = es2
        w1 = [load_w_bf16(S, d["cmp_w1"][kv], 16, 256, pfx + "w1%d" % kv) for kv in range(2)]
        w2 = [load_w_bf16(S, d["cmp_w2"][kv], 2, 64, pfx + "w2%d" % kv) for kv in range(2)]
        pe2 = [S.tile([128, 16], BF16, pfx + "pe%d" % kv) for kv in range(2)]
        for kv in range(2):
            S.dma('pool', lambda e, kv=kv: e.dma_start(out=pe2[kv][:], in_=d["c_pe2"][kv]), writes=[pe2[kv]])
        cbuf = S.tile([128, Tn], BF16, pfx + "cbuf")
        hid = S.tile([128, 2, NCp], BF16, pfx + "hid")
        bias = S.tile([128, 2], F32, pfx + "bias")
        psb = S.tile([128, 2], F32, pfx + "psb", psum=True)
        ph = [S.tile([128, 512], F32, pfx + "ph%d" % i, psum=True) for i in range(2)]
        pk = S.tile([128, 512], F32, pfx + "pk", psum=True)
        nph = 0
        for g in range(2):
            for kv in range(2):
                src = d["kc"][g * 2 + kv]
                S.dma('pool', lambda e, src=src: e.dma_start(out=cbuf[0:64, :], in_=src), writes=[cbuf])
                S.dma('pool', lambda e, src=src: e.dma_start(out=cbuf[64:128, 0:Tn - 1], in_=src[:, 1:Tn]), writes=[cbuf])
                for hc in range(2):
                    for j in range(16):
                        S.op('pe', lambda e, hc=hc, j=j, kv=kv: e.matmul(
                            psb[:, hc:hc + 1], lhsT=w1[kv][:, j, hc * 128:(hc + 1) * 128], rhs=pe2[kv][:, j:j + 1],
                            start=(j == 0), stop=(j == 15)), reads=[w1[kv], pe2[kv]], writes=[psb])
                S.op('dve', lambda e: e.tensor_copy(out=bias[:], in_=psb[:]), reads=[psb], writes=[bias])
                for hc in range(2):
                    for n0 in range(0, Nc, 512):
                        nn = min(512, Nc - n0)
                        p = ph[nph % 2]
                        nph += 1
                        for j in range(16):
                            st0 = 16 * n0 + 2 * j
                            S.op('pe', lambda e, hc=hc, j=j, kv=kv, p=p, nn=nn, st0=st0: e.matmul(
                                p[:, 0:nn], lhsT=w1[kv][:, j, hc * 128:(hc + 1) * 128],
                                rhs=cbuf[:, st0:st0 + 16 * (nn - 1) + 1:16], start=(j == 0), stop=(j == 15)),
                                reads=[w1[kv], cbuf], writes=[p])
                        S.op('act', lambda e, hc=hc, p=p, nn=nn, n0=n0: e.activation(
                            out=hid[:, hc, n0:n0 + nn], in_=p[:, 0:nn], func=AF.Silu, bias=bias[:, hc:hc + 1]),
                            reads=[p, bias], writes=[hid])
                if kv == 0:
                    for n0 in range(0, Nc, 512):
                        nn = min(512, Nc - n0)
                        for hc in range(2):
                            S.op('pe', lambda e, hc=hc, nn=nn, n0=n0: e.matmul(
                                pk[0:64, 0:nn], lhsT=w2[0][:, hc, :], rhs=hid[:, hc, n0:n0 + nn],
                                start=(hc == 0), stop=(hc == 1)), reads=[w2[0], hid], writes=[pk])
                        S.op('dve', lambda e, g=g, nn=nn, n0=n0: e.tensor_copy(out=kcT[0:64, g, n0:n0 + nn], in_=pk[0:64, 0:nn]),
                             reads=[pk], writes=[kcT])
                else:
                    for c in range((Nc + 127) // 128):
                        M = min(128, Nc - 128 * c)
                        for hc in range(2):
                            S.op('pe', lambda e, hc=hc, c=c, M=M: e.matmul(
                                pk[0:M, 0:64], lhsT=hid[:, hc, 128 * c:128 * c + M], rhs=w2[1][:, hc, :],
                                start=(hc == 0), stop=(hc == 1)), reads=[w2[1], hid], writes=[pk])
                        S.op('dve', lambda e, g=g, c=c, M=M: e.tensor_copy(out=vc_aug[0:M, g, c, 0:64], in_=pk[0:M, 0:64]),
                             reads=[pk], writes=[vc_aug])
        for g in range(2):
            S.dma('pool', lambda e, g=g: e.dma_start(out=kcT[64:69, g, :], in_=d["c_kca"]), writes=[kcT])
        S.op('pool', lambda e: e.memset(vc_aug[:, :, :, 64:65], 1.0), writes=[vc_aug])
        S.barrier()
        S.es = old


def phase_nsa(S, d, mix, Tn, ident_d, pfx):
    NB = Tn // 128
    NQ = NB // 4
    NCp = Tn // 16
    NCH = (NCp + 127) // 128
    kcT = S.tile([69, 2, NCp], BF16, pfx + "kcT")
    vc_aug = S.tile([128, 2, NCH, 65], BF16, pfx + "vca")
    phase_cmp(S, d, Tn, kcT, vc_aug, pfx + "c")
    ident = load_ident(S, ident_d, pfx)
    ksa = S.tile([69, 2, Tn], BF16, pfx + "ksa")
    vsa = S.tile([128, 2, NB, 65], BF16, pfx + "vsa")
    for g in range(2):
        S.dma('pool', lambda e, g=g: e.dma_start(out=ksa[:, g, :], in_=d["ksa"][g]), writes=[ksa])
        S.dma('pool', lambda e, g=g: e.dma_start(out=vsa[:, g, :, :], in_=d["vsa"][:, g, :, :]), writes=[vsa])
    OV = S.tile([128, NCH, 256], BF16, pfx + "OV")
    S.dma('pool', lambda e: e.dma_start(out=OV[:], in_=d["c_OV"]), writes=[OV])
    Em = S.tile([128, 64, 128], BF16, pfx + "E")
    S.dma('pool', lambda e: e.dma_start(out=Em[:], in_=d["c_E"]), writes=[Em])
    cmask = S.tile([128, NQ, 2, 128], BF16, pfx + "cmask")
    S.dma('pool', lambda e: e.dma_start(out=cmask[:], in_=d["c_cmask"]), writes=[cmask])
    dmask = S.tile([128, 4, 128], BF16, pfx + "dmask")
    S.dma('pool', lambda e: e.dma_start(out=dmask[:], in_=d["c_dmask"]), writes=[dmask])
    wmask = S.tile([128, 8, 128], BF16, pfx + "wmask")
    S.dma('pool', lambda e: e.dma_start(out=wmask[:], in_=d["c_wmask"]), writes=[wmask])
    impA = S.tile([128, NQ, 8], F32, pfx + "impA")
    impB = S.tile([128, NQ, 8], F32, pfx + "impB")
    S.dma('sp', lambda e: e.dma_start(out=impA[:], in_=d["c_impA"]), writes=[impA])
    S.dma('sp', lambda e: e.dma_start(out=impB[:], in_=d["c_impB"]), writes=[impB])
    g_bc = load_bc(S, d["nsa_out_g"], 512, pfx + "nsag")

    qa = [S.tile([69, 2, 512], BF16, pfx + "qa%d" % i) for i in range(2)]
    kw = [S.tile([69, 2, 1024], BF16, pfx + "kw%d" % i) for i in range(2)]
    vw = [S.tile([128, 2, 8, 65], BF16, pfx + "vw%d" % i) for i in range(2)]
    gts = S.tile([128, 24], F32, pfx + "gts")
    gs = S.tile([128, 8, 3], F32, pfx + "gs")
    pE = [S.tile([128, 512], BF16, pfx + "pE%d" % i) for i in range(3)]
    pTt = [S.tile([128, 512], BF16, pfx + "pT%d" % i) for i in range(3)]
    pcl = [S.tile([128, 512], F32, pfx + "pcl%d" % i) for i in range(2)]
    mk = [S.tile([128, 128], BF16, pfx + "mk%d" % i) for i in range(2)]
    imp = S.tile([128, 256], F32, pfx + "imp")
    wk = S.tile([128, 256], F32, pfx + "wk")
    mx = S.tile([128, 8], F32, pfx + "mx")
    sel = S.tile([128, 256], BF16, pfx + "sel")
    selT = S.tile([128, 2, 128], BF16, pfx + "selT")
    den = S.tile([128, 4], F32, pfx + "den")
    rden = S.tile([128, 4], F32, pfx + "rden")
    wgt = S.tile([128, 4], F32, pfx + "wgt")
    onsa = S.tile([128, 2, 4, 64], F32, pfx + "onsa")
    otmp = S.tile([128, 4, 64], F32, pfx + "otmp")
    oout = S.tile([128, 512], F32, pfx + "oout")
    scr = S.tile([128, 512], F32, pfx + "scr")
    ss = S.tile([128, 1], F32, pfx + "ss")
    rstd = S.tile([128, 1], F32, pfx + "rstd")

    ps_s = [S.tile([128, 512], F32, pfx + "pss%d" % i, psum=True) for i in range(2)]
    pmisc = S.es.enter_context(S.nc.psum_tensor(pfx + "pmisc", [128, 512], F32))
    ps_m = [T(pmisc[:, 0:128]), T(pmisc[:, 128:256])]
    pselT = S.tile([128, 2, 128], BF16, pfx + "pselT", psum=True)
    acc = [S.tile([128, 4, 65], F32, pfx + "acc%d" % i, psum=True) for i in range(2)]
    impP = S.tile([128, 4, 256], F32, pfx + "impP", psum=True)

    cnt = {'s': 0, 'a': 0, 'm': 0}

    def attend(QA, K_ap, K_t, M, V_ap, V_t, accT, first, last, mask_ap=None, mask_t=None, extra=None, clamp=False):
        n = cnt['s']
        cnt['s'] += 1
        ps = ps_s[n % 2]
        S.op('pe', lambda e: e.matmul(ps[0:M, :], lhsT=K_ap, rhs=QA[0], start=True, stop=True),
             reads=[K_t, QA[1]], writes=[ps])
        src, src_t = ps, ps
        if clamp:
            pc = pcl[n % 2]
            S.op('dve', lambda e: e.tensor_scalar_min(out=pc[0:M, :], in0=ps[0:M, :], scalar1=480.0), reads=[ps], writes=[pc])
            src, src_t = pc, pc
        if mask_ap is None:
            pT = pTt[n % 3]
            S.op('act', lambda e: e.activation(out=pT[0:M, :], in_=src[0:M, :], func=AF.Exp, scale=0.125),
                 reads=[src_t], writes=[pT])
        else:
            pe_ = pE[n % 3]
            pT = pTt[n % 3]
            S.op('act', lambda e: e.activation(out=pe_[0:M, :], in_=src[0:M, :], func=AF.Exp, scale=0.125),
                 reads=[src_t], writes=[pe_])
            S.op('dve', lambda e: e.tensor_tensor(
                out=pT[0:M, :].rearrange("p (h q) -> p h q", h=4), in0=pe_[0:M, :].rearrange("p (h q) -> p h q", h=4),
                in1=mask_ap.unsqueeze(1).to_broadcast([M, 4, 128]), op=ALU.mult),
                reads=[pe_, mask_t], writes=[pT])
        for r in range(4):
            S.op('pe', lambda e, r=r: e.matmul(accT[:, r, :], lhsT=pT[0:M, r * 128:(r + 1) * 128], rhs=V_ap,
                                               start=(first and r == 0), stop=last), reads=[pT, V_t], writes=[accT])
        if extra is not None:
            extra(pT, M, first, last)

    def combine(accT, g, br, first_branch):
        S.op('dve', lambda e: e.tensor_scalar_max(out=den[:], in0=accT[:, :, 64], scalar1=1e-30), reads=[accT], writes=[den])
        S.op('dve', lambda e: e.reciprocal(out=rden[:], in_=den[:]), reads=[den], writes=[rden])
        S.op('dve', lambda e: e.tensor_tensor(out=wgt[:], in0=rden[:], in1=gs[:, g * 4:(g + 1) * 4, br], op=ALU.mult),
             reads=[rden, gs], writes=[wgt])
        if first_branch:
            S.op('dve', lambda e: e.tensor_tensor(out=onsa[:, g, :, :], in0=accT[:, :, 0:64],
                                                  in1=wgt[:].unsqueeze(2).to_broadcast([128, 4, 64]), op=ALU.mult),
                 reads=[accT, wgt], writes=[onsa])
        else:
            S.op('dve', lambda e: e.tensor_tensor(out=otmp[:], in0=accT[:, :, 0:64],
                                                  in1=wgt[:].unsqueeze(2).to_broadcast([128, 4, 64]), op=ALU.mult),
                 reads=[accT, wgt], writes=[otmp])
            S.op('pool', lambda e: e.tensor_tensor(out=onsa[:, g, :, :], in0=onsa[:, g, :, :], in1=otmp[:], op=ALU.add),
                 reads=[onsa, otmp], writes=[onsa])

    for m in range(NQ):
        imax = 4 * m + 3
        qat = qa[m % 2]
        kwt = kw[m % 2]
        vwt = vw[m % 2]
        w0 = 4 if m == 0 else 0
        kbw0 = 4 * m - 4 + w0
        nwc = 8 - w0
        for g in range(2):
            S.dma('pool', lambda e, g=g, m=m, qat=qat: e.dma_start(out=qat[:, g, :], in_=d["qa"][g, m]), writes=[qat])
            S.dma('pool', lambda e, g=g, kwt=kwt, kbw0=kbw0, nwc=nwc, w0=w0: e.dma_start(
                out=kwt[:, g, w0 * 128:(w0 + nwc) * 128], in_=d["kwa"][g][:, kbw0 * 128:(kbw0 + nwc) * 128]), writes=[kwt])
            S.dma('pool', lambda e, g=g, vwt=vwt, kbw0=kbw0, nwc=nwc, w0=w0: e.dma_start(
                out=vwt[:, g, w0:w0 + nwc, :], in_=d["vwa"][:, g, kbw0:kbw0 + nwc, :]), writes=[vwt])
        S.dma('sp', lambda e, m=m: e.dma_start(out=gts[:], in_=d["gt"][m]), writes=[gts])
        S.op('act', lambda e: e.activation(out=gs[:].rearrange("p h t -> p (h t)"), in_=gts[:], func=AF.Sigmoid),
             reads=[gts], writes=[gs])
        for g in range(2):
            QA = (qat[:, g, :], qat)
            nk = 8 * imax + 7
            clast = (nk - 1) // 128
            accC = acc[cnt['a'] % 2]
            cnt['a'] += 1
            for c in range(clast + 1):
                M = min(128, nk - 128 * c)
                mask_ap = None
                if c == clast:
                    mask_ap = cmask[0:M, m, 1, :]
                elif c == clast - 1:
                    mask_ap = cmask[0:M, m, 0, :]

                def extra(pT, M, first, last, c=c):
                    for r in range(4):
                        S.op('pe', lambda e, r=r: e.matmul(impP[:, r, :], lhsT=pT[0:M, r * 128:(r + 1) * 128],
                                                           rhs=OV[0:M, c, :], start=(first and r % 2 == 0), stop=last),
                             reads=[pT, OV], writes=[impP])

                attend(QA, kcT[:, g, 128 * c:128 * c + M], kcT, M, vc_aug[0:M, g, c, :], vc_aug, accC,
                       c == 0, c == clast, mask_ap, cmask if mask_ap is not None else None, extra,
                       clamp=(mask_ap is not None))
            S.op('dve', lambda e, accC=accC: e.tensor_scalar_max(out=den[:], in0=accC[:, :, 64], scalar1=1e-30),
                 reads=[accC], writes=[den])
            S.op('dve', lambda e: e.reciprocal(out=rden[:], in_=den[:]), reads=[den], writes=[rden])
            S.op('pool', lambda e: e.memset(imp[:], -1.0), writes=[imp])
            nv = 8 * m + 8
            S.op('dve', lambda e, nv=nv: e.tensor_scalar_mul(out=imp[:, 0:nv], in0=impP[:, 0, 0:nv], scalar1=rden[:, 0:1]),
                 reads=[impP, rden], writes=[imp])
            for r in range(1, 4):
                S.op('dve', lambda e, r=r, nv=nv: e.scalar_tensor_tensor(
                    out=imp[:, 0:nv], in0=impP[:, r, 0:nv], scalar=rden[:, r:r + 1], in1=imp[:, 0:nv],
                    op0=ALU.mult, op1=ALU.add), reads=[impP, rden, imp], writes=[imp])
            if m > 0:
                S.op('dve', lambda e: e.memset(imp[:, 0:1], 1e4), writes=[imp])
            S.op('dve', lambda e, nv=nv, m=m: e.tensor_tensor(out=imp[:, nv - 8:nv], in0=imp[:, nv - 8:nv], in1=impA[:, m, :],
                                                             op=ALU.mult), reads=[imp, impA], writes=[imp])
            S.op('dve', lambda e, nv=nv, m=m: e.tensor_tensor(out=imp[:, nv - 8:nv], in0=imp[:, nv - 8:nv], in1=impB[:, m, :],
                                                             op=ALU.add), reads=[imp, impB], writes=[imp])
            S.op('dve', lambda e: e.max(out=mx[:], in_=imp[:]), reads=[imp], writes=[mx])
            S.op('dve', lambda e: e.match_replace(out=wk[:], in_to_replace=mx[:], in_values=imp[:], imm_value=-2.0),
                 reads=[imp, mx], writes=[wk])
            S.op('dve', lambda e: e.max(out=mx[:], in_=wk[:]), reads=[wk], writes=[mx])
            S.op('dve', lambda e: e.match_replace(out=wk[:], in_to_replace=mx[:], in_values=wk[:], imm_value=-2.0),
                 reads=[wk, mx], writes=[wk])
            S.op('dve', lambda e: e.tensor_tensor(out=sel[:], in0=imp[:], in1=wk[:], op=ALU.not_equal),
                 reads=[imp, wk], writes=[sel])
            njc = (2 * imax + 1) // 128 + 1
            for jc in range(njc):
                S.op('pe', lambda e, jc=jc: e.transpose(out=pselT[:, jc, :], in_=sel[:, jc * 128:(jc + 1) * 128],
                                                        identity=ident[:]), reads=[sel, ident], writes=[pselT])
            S.op('act', lambda e, njc=njc: e.activation(out=selT[:, 0:njc, :], in_=pselT[:, 0:njc, :], func=AF.Copy),
                 reads=[pselT], writes=[selT])
            combine(accC, g, 0, True)
            accS = acc[cnt['a'] % 2]
            cnt['a'] += 1
            for kb in range(imax + 1):
                pm = ps_m[cnt['m'] % 2]
                v = kb % 64
                jc = kb // 64
                S.op('pe', lambda e, v=v, jc=jc, pm=pm: e.matmul(pm[:], lhsT=Em[:, v, :], rhs=selT[:, jc, :],
                                                                start=True, stop=True), reads=[Em, selT], writes=[pm])
                tail = kb >= 4 * m
                if tail:
                    mkt = mk[cnt['m'] % 2]
                    jj = kb - 4 * m
                    S.op('dve', lambda e, pm=pm, mkt=mkt, jj=jj: e.tensor_tensor(out=mkt[:], in0=pm[:], in1=dmask[:, jj, :],
                                                                                op=ALU.mult), reads=[pm, dmask], writes=[mkt])
                    mask_ap, mask_t = mkt[:], mkt
                else:
                    mask_ap, mask_t = pm[:], pm
                cnt['m'] += 1
                attend(QA, ksa[:, g, kb * 128:(kb + 1) * 128], ksa, 128, vsa[:, g, kb, :], vsa, accS,
                       kb == 0, kb == imax, mask_ap, mask_t, clamp=tail)
            combine(accS, g, 1, False)
            accW = acc[cnt['a'] % 2]
            cnt['a'] += 1
            for w in range(w0, 8):
                attend(QA, kwt[:, g, w * 128:(w + 1) * 128], kwt, 128, vwt[:, g, w, :], vwt, accW,
                       w == w0, w == 7, wmask[:, w, :], wmask, clamp=(w >= 4))
            combine(accW, g, 2, False)
        of = onsa[:].rearrange("p g h d -> p (g h d)")
        rms_stats(S, of, onsa, 512, scr, ss, rstd)
        S.op('dve', lambda e: e.scalar_tensor_tensor(out=oout[:], in0=onsa[:].rearrange("p g h d -> p (g h d)"),
                                                     scalar=rstd[:], in1=g_bc[:], op0=ALU.mult, op1=ALU.mult),
             reads=[onsa, rstd, g_bc], writes=[oout])
        S.dma('sp', lambda e, m=m: e.dma_start(out=mix[m * 128:(m + 1) * 128, 0:512], in_=oout[:]), reads=[oout])


def phase_post(S, d, x1, mix, x3, ident_d, ntok, pfx):
    wout = load_w_bf16(S, d["w_out"], 8, D, pfx + "wout")
    wq = load_w_bf16(S, d["xa_wq"], 8, D, pfx + "wq")
    wo = load_w_bf16(S, d["xa_wo"], 8, D, pfx + "wo")
    g_mixpost = load_bc(S, d["mix_post_g"], D, pfx + "g1")
    g_xapre = load_bc(S, d["xa_pre_g"], D, pfx + "g2")
    g_xapost = load_bc(S, d["xa_post_g"], D, pfx + "g3")
    ident = load_ident(S, ident_d, pfx)
    NT = NormT(S, ident, pfx)
    kmT = S.tile([128, 8, 256], BF16, pfx + "kmT")
    vm = S.tile([128, 2, 4, 256], BF16, pfx + "vm")
    ones_c = S.tile([128, 1], BF16, pfx + "ones")
    S.op('pool', lambda e: e.memset(ones_c[:], 1.0), writes=[ones_c])
    pa = [S.tile([128, 512], F32, pfx + "pa%d" % i, psum=True) for i in range(2)]
    with ExitStack() as es2:
        old = S.es
        S.es = es2
        wk_ = load_w_bf16(S, d["xa_wk"], 8, D, pfx + "wk")
        wv_ = load_w_bf16(S, d["xa_wv"], 8, D, pfx + "wv")
        g_mem = load_bc(S, d["xa_mem_g"], D, pfx + "gm")
        mt = S.tile([128, 2, D], F32, pfx + "mt")
        mnT = S.tile([128, 8, 256], BF16, pfx + "mnT")
        S.dma('sp', lambda e: e.dma_start(out=mt[:], in_=d["mem"].rearrange("(n p) d -> p n d", p=128)), writes=[mt])
        for b in range(2):
            NT.norm_T(mt[:, b, :], mt, g_mem, mnT, b * 128)
        for dc in range(8):
            p = pa[dc % 2]
            for c in range(8):
                S.op('pe', lambda e, c=c, dc=dc, p=p: e.matmul(p[:, 0:256], lhsT=wk_[:, c, dc * 128:(dc + 1) * 128],
                                                              rhs=mnT[:, c, :], start=(c == 0), stop=(c == 7)),
                     reads=[wk_, mnT], writes=[p])
            S.op('dve', lambda e, dc=dc, p=p: e.tensor_copy(out=kmT[:, dc, :], in_=p[:, 0:256]), reads=[p], writes=[kmT])
        n = 0
        for mc in range(2):
            for hf in range(2):
                p = pa[n % 2]
                n += 1
                for c in range(8):
                    S.op('pe', lambda e, c=c, mc=mc, hf=hf, p=p: e.matmul(
                        p[:], lhsT=mnT[:, c, mc * 128:(mc + 1) * 128], rhs=wv_[:, c, hf * 512:(hf + 1) * 512],
                        start=(c == 0), stop=(c == 7)), reads=[wv_, mnT], writes=[p])
                S.op('dve', lambda e, mc=mc, hf=hf, p=p: e.tensor_copy(
                    out=vm[:, mc, 2 * hf:2 * hf + 2, 0:256], in_=p[:].rearrange("p (h d) -> p h d", h=2)),
                    reads=[p], writes=[vm])
        S.barrier()
        S.es = old
    xs = [S.tile([128, D], F32, pfx + "x%d" % i) for i in range(2)]
    ms = [S.tile([128, D], F32, pfx + "m%d" % i) for i in range(2)]
    mb = S.tile([128, D], BF16, pfx + "mb")
    mixT = S.tile([128, 8, 128], BF16, pfx + "mixT")
    ysb = S.tile([128, D], F32, pfx + "ysb")
    x2 = S.tile([128, D], F32, pfx + "x2")
    hT = S.tile([128, 8, 128], BF16, pfx + "hT")
    qT = S.tile([128, 8, 128], BF16, pfx + "qT")
    pT = [S.tile([128, 512], BF16, pfx + "pT%d" % i) for i in range(2)]
    ao = S.tile([128, 4, 256], F32, pfx + "ao")
    den = S.tile([128, 4], F32, pfx + "den")
    ab = S.tile([128, D], BF16, pfx + "ab")
    aT = S.tile([128, 8, 128], BF16, pfx + "aT")
    osb = [S.tile([128, D], F32, pfx + "o%d" % i) for i in range(2)]
    pq = S.tile([128, 8, 128], F32, pfx + "pq", psum=True)
    pav = S.tile([128, 4, 256], F32, pfx + "pav", psum=True)
    pden = S.tile([128, 4], F32, pfx + "pden", psum=True)
    for b in range(ntok // 128):
        xb = xs[b % 2]
        mbk = ms[b % 2]
        S.dma('sp', lambda e, b=b, xb=xb: e.dma_start(out=xb[:], in_=x1[b * 128:(b + 1) * 128, :]), writes=[xb])
        S.dma('sp', lambda e, b=b, mbk=mbk: e.dma_start(out=mbk[:], in_=mix[b * 128:(b + 1) * 128, :]), writes=[mbk])
        S.op('dve', lambda e, mbk=mbk: e.tensor_copy(out=mb[:], in_=mbk[:]), reads=[mbk], writes=[mb])
        NT.transp(mb, mixT, 0)
        for hf in range(2):
            p = pa[hf]
            for c in range(8):
                S.op('pe', lambda e, c=c, hf=hf, p=p: e.matmul(p[:], lhsT=mixT[:, c, :], rhs=wout[:, c, hf * 512:(hf + 1) * 512],
                                                              start=(c == 0), stop=(c == 7)), reads=[mixT, wout], writes=[p])
            S.op('act', lambda e, hf=hf, p=p: e.activation(out=ysb[:, hf * 512:(hf + 1) * 512], in_=p[:], func=AF.Copy),
                 reads=[p], writes=[ysb])
        NT.norm(ysb[:], ysb, g_mixpost, x2[:], x2)
        S.op('pool', lambda e, xb=xb: e.tensor_tensor(out=x2[:], in0=x2[:], in1=xb[:], op=ALU.add), reads=[x2, xb], writes=[x2])
        NT.norm_T(x2[:], x2, g_xapre, hT, 0)
        for dc in range(8):
            for c in range(8):
                S.op('pe', lambda e, c=c, dc=dc: e.matmul(pq[:, dc, :], lhsT=wq[:, c, dc * 128:(dc + 1) * 128], rhs=hT[:, c, :],
                                                         start=(c == 0), stop=(c == 7)), reads=[wq, hT], writes=[pq])
        S.op('act', lambda e: e.activation(out=qT[:], in_=pq[:], func=AF.Copy), reads=[pq], writes=[qT])
        for mc in range(2):
            p = pa[mc]
            for h in range(4):
                for c2 in range(2):
                    S.op('pe', lambda e, h=h, c2=c2, mc=mc, p=p: e.matmul(
                        p[:, h * 128:(h + 1) * 128], lhsT=kmT[:, 2 * h + c2, mc * 128:(mc + 1) * 128], rhs=qT[:, 2 * h + c2, :],
                        start=(c2 == 0), stop=(c2 == 1)), reads=[kmT, qT], writes=[p])
            pt_ = pT[mc]
            S.op('act', lambda e, p=p, pt_=pt_: e.activation(out=pt_[:], in_=p[:], func=AF.Exp, scale=1.0 / 16.0),
                 reads=[p], writes=[pt_])
        for h in range(4):
            for mc in range(2):
                S.op('pe', lambda e, h=h, mc=mc: e.matmul(pav[:, h, :], lhsT=pT[mc][:, h * 128:(h + 1) * 128], rhs=vm[:, mc, h, :],
                                                         start=(mc == 0), stop=(mc == 1)), reads=[pT[mc], vm], writes=[pav])
        for h in range(4):
            for mc in range(2):
                S.op('pe', lambda e, h=h, mc=mc: e.matmul(pden[:, h:h + 1], lhsT=pT[mc][:, h * 128:(h + 1) * 128], rhs=ones_c[:],
                                                         start=(mc == 0), stop=(mc == 1)), reads=[pT[mc], ones_c], writes=[pden])
        S.op('dve', lambda e: e.reciprocal(out=den[:], in_=pden[:]), reads=[pden], writes=[den])
        S.op('act', lambda e: e.activation(out=ao[:], in_=pav[:], func=AF.Copy), reads=[pav], writes=[ao])
        S.op('dve', lambda e: e.tensor_tensor(out=ab[:].rearrange("p (h d) -> p h d", h=4), in0=ao[:],
                                              in1=den[:].unsqueeze(2).to_broadcast([128, 4, 256]), op=ALU.mult),
             reads=[ao, den], writes=[ab])
        NT.transp(ab, aT, 0)
        for hf in range(2):
            p = pa[hf]
            for c in range(8):
                S.op('pe', lambda e, c=c, hf=hf, p=p: e.matmul(p[:], lhsT=aT[:, c, :], rhs=wo[:, c, hf * 512:(hf + 1) * 512],
                                                              start=(c == 0), stop=(c == 7)), reads=[aT, wo], writes=[p])
            S.op('act', lambda e, hf=hf, p=p: e.activation(out=ysb[:, hf * 512:(hf + 1) * 512], in_=p[:], func=AF.Copy),
                 reads=[p], writes=[ysb])
        o = osb[b % 2]
        NT.norm(ysb[:], ysb, g_xapost, o[:], o)
        S.op('pool', lambda e, o=o: e.tensor_tensor(out=o[:], in0=o[:], in1=x2[:], op=ALU.add), reads=[o, x2], writes=[o])
        S.dma('sp', lambda e, b=b, o=o: e.dma_start(out=x3[b * 128:(b + 1) * 128, :], in_=o[:]), reads=[o])


def run_phase(S, fn):
    with ExitStack() as es2:
        old = S.es
        S.es = es2
        fn()
        S.barrier()
        S.es = old


def din(nc, name, shape):
    return nc.dram_tensor(name, list(shape), F32, kind="ExternalInput").ap()


def build_A(ntok):
    nc = bass.Bass("TRN2", target_bir_lowering=False)
    x = din(nc, "x", [ntok, D])
    wg = din(nc, "wg", [D, DFF]); wu = din(nc, "wu", [D, DFF]); wd = din(nc, "wd", [DFF, D])
    gpre = din(nc, "gpre", [1, D]); gpost = din(nc, "gpost", [1, D])
    w_in = din(nc, "w_in", [D, INW]); gmix = din(nc, "gmix", [1, D])
    ident_d = din(nc, "ident", [128, 128])
    x1 = nc.dram_tensor("x1", [ntok, D], F32, kind="ExternalOutput").ap()
    proj = nc.dram_tensor("proj", [ntok, INW], F32, kind="ExternalOutput").ap()
    with ExitStack() as es:
        S = Sched(nc, es)
        run_phase(S, lambda: phase_ffn(S, x, x1, wg, wu, wd, gpre, gpost, ident_d, ntok, "f"))
        run_phase(S, lambda: phase_proj(S, x1, proj, w_in, gmix, ident_d, ntok, "p"))
        S.finish()
        S.emit()
    return nc


def mixer_inputs(nc, Tn):
    NB = Tn // 128
    NQ = NB // 4
    NCp = Tn // 16
    NCH = (NCp + 127) // 128
    shapes = dict(
        qa=[2, NQ, 69, 512], ksa=[2, 69, Tn], kwa=[2, 69, Tn], vsa=[128, 2, NB, 65], vwa=[128, 2, NB, 65],
        kc=[4, 64, Tn], gt=[NQ, 128, 24], qrT=[NQ, 128, 4, 128], krT=[NQ, 128, 4, 128], gr=[NQ, 128, 512],
        kr_all=[128, NB, 512], vr_all=[128, NB, 512],
        cmp_w1=[2, 2048, 256], cmp_w2=[2, 256, 64], c_pe2=[2, 128, 16], c_kca=[5, NCp],
        c_OV=[128, NCH, 256], c_E=[128, 64, 128], c_cmask=[128, NQ, 2, 128],
        c_dmT=[128, 4, 128], c_qdecb=[128, 4, 128], c_kdec=[128, 4], c_rsel=[128, 4], vr_own=[NQ, 128, 512],
        c_dmask=[128, 4, 128], c_wmask=[128, 8, 128], c_impA=[128, NQ, 8], c_impB=[128, NQ, 8],
        nsa_out_g=[1, 512], ret_gn_g=[1, 512],
        w_out=[D, D], xa_wq=[D, D], xa_wk=[D, D], xa_wv=[D, D], xa_wo=[D, D],
        mix_post_g=[1, D], xa_pre_g=[1, D], xa_mem_g=[1, D], xa_post_g=[1, D], mem=[256, D],
    )
    return {k: din(nc, k, v) for k, v in shapes.items()}


def build_B(Tn):
    ntok = Tn // 4
    nc = bass.Bass("TRN2", target_bir_lowering=False)
    d = mixer_inputs(nc, Tn)
    x1 = din(nc, "x1", [ntok, D])
    wg = din(nc, "wg", [D, DFF]); wu = din(nc, "wu", [D, DFF]); wd = din(nc, "wd", [DFF, D])
    gpre = din(nc, "gpre", [1, D]); gpost = din(nc, "gpost", [1, D])
    ident_d = din(nc, "ident", [128, 128])
    mix = nc.dram_tensor("mix", [ntok, D], F32, kind="ExternalOutput").ap()
    x3 = nc.dram_tensor("x3", [ntok, D], F32, kind="ExternalOutput").ap()
    xo = nc.dram_tensor("xo", [ntok, D], F32, kind="ExternalOutput").ap()
    with ExitStack() as es:
        S = Sched(nc, es)
        run_phase(S, lambda: phase_ret(S, d, mix, Tn, "r"))
        run_phase(S, lambda: phase_nsa(S, d, mix, Tn, ident_d, "n"))
        run_phase(S, lambda: phase_post(S, d, x1, mix, x3, ident_d, ntok, "q"))
        run_phase(S, lambda: phase_ffn(S, x3, xo, wg, wu, wd, gpre, gpost, ident_d, ntok, "g"))
        S.finish()
        S.emit()
    return nc


def slopes():
    return np.array([2.0 ** (-(h + 1)) for h in range(8)], dtype=np.float64)


def host_consts(Tn, rank):
    NB = Tn // 128
    NQ = NB // 4
    NCp = Tn // 16
    NCH = (NCp + 127) // 128
    c = {}
    n = np.arange(NCp)
    c["c_kca"] = np.stack([np.ones(NCp), np.ones(NCp), 2048.0 * (n // 128), 16.0 * (n % 128), np.ones(NCp)]).astype(np.float32)
    OV = np.zeros((NCH * 128, 256), np.float32)
    for nn in range(NCp):
        j0 = nn // 4
        if j0 >= 256:
            continue
        if nn % 4 < 3:
            OV[nn, j0] = 1.0
        else:
            OV[nn, j0] = 0.5
            if j0 + 1 < 256:
                OV[nn, j0 + 1] = 0.5
    c["c_OV"] = np.ascontiguousarray(OV.reshape(NCH, 128, 256).transpose(1, 0, 2))
    E = np.zeros((128, 64, 128), np.float32)
    key = np.arange(128)
    for v in range(64):
        E[2 * v + key // 64, v, key] = 1.0
    c["c_E"] = E
    cm = np.zeros((128, NQ, 2, 128), np.float32)
    nl = np.arange(128)[:, None]
    ql = np.arange(128)[None, :]
    impA = np.zeros((128, NQ, 8), np.float32)
    impB = np.zeros((128, NQ, 8), np.float32)
    for m in range(NQ):
        i = 4 * m + rank
        nk = 8 * (4 * m + 3) + 7
        clast = (nk - 1) // 128
        for s, cc in ((0, clast - 1), (1, clast)):
            if cc < 0:
                continue
            cm[:, m, s, :] = (16 * (128 * cc + nl) + 31 <= 128 * i + ql).astype(np.float32)
        cur = 2 * i + (np.arange(128) >= 64).astype(np.int64)
        for jj in range(8):
            j = 8 * m + jj
            forced = (j == 0) | (j == cur) | (j == cur - 1)
            valid = j <= cur
            impA[:, m, jj] = np.where(forced, 0.0, np.where(valid, 1.0, 0.0))
            impB[:, m, jj] = np.where(forced, 1e4, np.where(valid, 0.0, -1.0))
    c["c_cmask"] = cm
    c["c_impA"] = impA
    c["c_impB"] = impB
    tri0 = (nl <= ql).astype(np.float32)
    dmk = np.zeros((128, 4, 128), np.float32)
    for jj in range(4):
        dmk[:, jj, :] = 1.0 if jj < rank else (tri0 if jj == rank else 0.0)
    c["c_dmask"] = dmk
    wm = np.zeros((128, 8, 128), np.float32)
    for w in range(8):
        dist = 128 * (rank + 4 - w) + ql - nl
        wm[:, w, :] = ((dist >= 0) & (dist < 512)).astype(np.float32)
    c["c_wmask"] = wm
    rs = np.zeros((128, 4), np.float32)
    rs[:, rank] = 1.0
    c["c_rsel"] = rs
    gam = np.array(ret_gammas(), np.float64)
    nn_ = np.arange(128)
    dm = np.zeros((128, 4, 128), np.float64)
    for h in range(4):
        diff = nn_[None, :] - nn_[:, None]
        dm[:, h, :] = np.where(diff >= 0, gam[h] ** np.maximum(diff, 0), 0.0) * (128 ** -0.5)
    c["c_dmT"] = dm.astype(np.float32)
    qd = np.zeros((128, 4, 128), np.float64)
    for h in range(4):
        qd[:, h, :] = (gam[h] ** (nn_ + 1.0))[None, :]
    c["c_qdecb"] = qd.astype(np.float32)
    kd = np.zeros((128, 4), np.float64)
    for h in range(4):
        kd[:, h] = gam[h] ** (127.0 - nn_) * (128 ** -0.5)
    c["c_kdec"] = kd.astype(np.float32)
    return c


def host_mixer_inputs(proj_b, Tn, rank, consts):
    NB = Tn // 128
    NQ = NB // 4
    d = dict(consts)
    sl = slopes()
    own = [4 * m + rank for m in range(NQ)]
    pos = np.arange(Tn)
    kaug = np.stack([np.ones(Tn), np.ones(Tn), 128.0 * (pos // 128), 1.0 * (pos % 128), np.zeros(Tn)]).astype(np.float32)
    qa = np.zeros((2, NQ, 69, 512), np.float32)
    ql = np.arange(128)
    for g in range(2):
        for m, i in enumerate(own):
            blk = proj_b[i * 128:(i + 1) * 128, g * 256:(g + 1) * 256].reshape(128, 4, 64)
            qa[g, m, 0:64] = blk.transpose(2, 1, 0).reshape(64, 512)
            for r in range(4):
                s = sl[g * 4 + r]
                cols = slice(r * 128, (r + 1) * 128)
                qa[g, m, 64, cols] = -8.0 * s * 128.0 * i
                qa[g, m, 65, cols] = -8.0 * s * ql
                qa[g, m, 66, cols] = 8.0 * s
                qa[g, m, 67, cols] = 8.0 * s
                qa[g, m, 68, cols] = 124.0 * s
    d["qa"] = qa

    def kT_aug(c0):
        out = np.zeros((2, 69, Tn), np.float32)
        for g in range(2):
            out[g, 0:64] = proj_b[:, c0 + g * 64:c0 + (g + 1) * 64].T
            out[g, 64:69] = kaug
        return out

    def v_aug(c0):
        out = np.ones((128, 2, NB, 65), np.float32)
        for g in range(2):
            out[:, g, :, 0:64] = proj_b[:, c0 + g * 64:c0 + (g + 1) * 64].reshape(NB, 128, 64).transpose(1, 0, 2)
        return out

    d["ksa"] = kT_aug(768)
    d["vsa"] = v_aug(896)
    d["kwa"] = kT_aug(1024)
    d["vwa"] = v_aug(1152)
    kc = np.zeros((4, 64, Tn), np.float32)
    for g in range(2):
        kc[g * 2 + 0] = proj_b[:, 512 + g * 64:512 + (g + 1) * 64].T
        kc[g * 2 + 1] = proj_b[:, 640 + g * 64:640 + (g + 1) * 64].T
    d["kc"] = kc
    pb = proj_b.reshape(NB, 128, INW)
    d["gt"] = np.ascontiguousarray(pb[own][:, :, 1280:1304])
    d["qrT"] = np.ascontiguousarray(pb[own][:, :, 1304:1816].reshape(NQ, 128, 4, 128).transpose(0, 3, 2, 1))
    d["krT"] = np.ascontiguousarray(pb[own][:, :, 1816:2328].reshape(NQ, 128, 4, 128).transpose(0, 3, 2, 1))
    d["gr"] = np.ascontiguousarray(pb[own][:, :, 2840:3352])
    d["vr_own"] = np.ascontiguousarray(pb[own][:, :, 2328:2840])
    d["kr_all"] = np.ascontiguousarray(pb[:, :, 1816:2328].transpose(1, 0, 2))
    d["vr_all"] = np.ascontiguousarray(pb[:, :, 2328:2840].transpose(1, 0, 2))
    return d


def pe2_layout(cmp_pe_l):
    return np.ascontiguousarray(cmp_pe_l.reshape(2, 16, 2, 64).transpose(0, 2, 3, 1).reshape(2, 128, 16))


_CACHE = {}


def get_prog(kind, *args):
    key = (kind,) + args
    if key not in _CACHE:
        _CACHE[key] = build_A(*args) if kind == "A" else build_B(*args)
    return _CACHE[key]


def shard_tokens(xb_all, Tn):
    B = xb_all.shape[0]
    NB = Tn // 128
    out = []
    for c in range(NCORES):
        b, k = c // 4, c % 4
        blk = xb_all[b].reshape(NB, 128, -1)[k::4]
        out.append(np.ascontiguousarray(blk.reshape(-1, xb_all.shape[-1])))
    return out


def unshard_tokens(parts, Tn, W):
    NB = Tn // 128
    out = np.zeros((2, NB, 128, W), np.float32)
    for c in range(NCORES):
        b, k = c // 4, c % 4
        out[b, k::4] = parts[c].reshape(NB // 4, 128, W)
    return out.reshape(2, Tn, W)


def run_model(inp, Tn, depth=2, debug=None):
    ident = np.eye(128, dtype=np.float32)
    x_parts = shard_tokens(np.asarray(inp["x"], np.float32), Tn)
    ntok = Tn // 4
    f = lambda a: np.ascontiguousarray(np.asarray(a, np.float32))
    for l in range(depth):
        progA = get_prog("A", ntok)
        maps = []
        for c in range(NCORES):
            maps.append(dict(x=x_parts[c], wg=f(inp["ffn1_w_gate"][l]), wu=f(inp["ffn1_w_up"][l]), wd=f(inp["ffn1_w_down"][l]),
                             gpre=f(inp["ffn1_pre_g"][l][None]), gpost=f(inp["ffn1_post_g"][l][None]),
                             w_in=f(inp["w_in"][l]), gmix=f(inp["mix_pre_g"][l][None]), ident=ident))
        res = run_bass_kernel_spmd(progA, maps, core_ids=list(range(NCORES)))
        x1_parts = [r["x1"] for r in res.results]
        proj = unshard_tokens([r["proj"] for r in res.results], Tn, INW)
        if debug is not None:
            debug["x1_%d" % l] = unshard_tokens(x1_parts, Tn, D)
            debug["proj_%d" % l] = proj
        progB = get_prog("B", Tn)
        maps = []
        for c in range(NCORES):
            b, rank = c // 4, c % 4
            d = host_mixer_inputs(proj[b], Tn, rank, host_consts(Tn, rank))
            d.update(cmp_w1=f(inp["cmp_w1"][l]), cmp_w2=f(inp["cmp_w2"][l]), c_pe2=pe2_layout(f(inp["cmp_pe"][l])),
                     nsa_out_g=f(inp["nsa_out_g"][l][None]), ret_gn_g=f(inp["ret_gn_g"][l][None]),
                     w_out=f(inp["w_out"][l]), xa_wq=f(inp["xa_wq"][l]), xa_wk=f(inp["xa_wk"][l]),
                     xa_wv=f(inp["xa_wv"][l]), xa_wo=f(inp["xa_wo"][l]),
                     mix_post_g=f(inp["mix_post_g"][l][None]), xa_pre_g=f(inp["xa_pre_g"][l][None]),
                     xa_mem_g=f(inp["xa_mem_g"][l][None]), xa_post_g=f(inp["xa_post_g"][l][None]),
                     mem=f(inp["mem"][b]), x1=x1_parts[c],
                     wg=f(inp["ffn2_w_gate"][l]), wu=f(inp["ffn2_w_up"][l]), wd=f(inp["ffn2_w_down"][l]),
                     gpre=f(inp["ffn2_pre_g"][l][None]), gpost=f(inp["ffn2_post_g"][l][None]), ident=ident)
            maps.append(d)
        res = run_bass_kernel_spmd(progB, maps, core_ids=list(range(NCORES)))
        outs = list(res.results)
        del maps
        if debug is not None:
            debug["mix_%d" % l] = unshard_tokens([o["mix"] for o in outs], Tn, D)
            debug["x3_%d" % l] = unshard_tokens([o["x3"] for o in outs], Tn, D)
        x_parts = [o["xo"] for o in outs]
    return unshard_tokens(x_parts, Tn, D)


def kernel(**inputs):
    return run_model(inputs, 16384).astype(np.float32)
```
